# Optimizing a Trainium2 kernel written in Bass

```python
import jax, jax.numpy as jnp
from jax import lax
import numpy as np

D_MODEL = 1024
BATCH = 8
SEQ = 4096
DEPTH = 2

N_A_LAYERS = (DEPTH + 1) // 2
N_B_LAYERS = DEPTH - N_A_LAYERS
HGRN_DK = 128
HGRN_DV = 128
HGRN_HEADS = D_MODEL // HGRN_DK
HGRN_WIDTH = HGRN_HEADS * HGRN_DK
HGRN_CHUNK = 64
FOX_HEAD_DIM = 64
FOX_HEADS = D_MODEL // FOX_HEAD_DIM
FOX_WIDTH = FOX_HEADS * FOX_HEAD_DIM
FOX_QBLOCK = 128
FOX_GATE_BIAS_OFFSET = 3.0
D_FF = ((8 * D_MODEL // 3 + 255) // 256) * 256
PLE_DIM = 256
NORM_EPS = 1e-6

kernel_name = "yoco_hgrn2_fox_macaron_hybrid"


def rmsnorm(x, g):
    x32 = x.astype(jnp.float32)
    y = x32 * lax.rsqrt(jnp.mean(jnp.square(x32), axis=-1, keepdims=True) + NORM_EPS)
    return (y * g.astype(jnp.float32)).astype(x.dtype)


def swiglu(x, w_in, w_out):
    gate, up = jnp.split(x @ w_in, 2, axis=-1)
    return (jax.nn.silu(gate) * up) @ w_out


def hgrn2_chunked_scan(q, log_f, k, v):
    b_, s_, h_, dk = q.shape
    dv = v.shape[-1]
    n_chunks = s_ // HGRN_CHUNK

    def to_chunks(t):
        return t.astype(jnp.float32).reshape(b_, n_chunks, HGRN_CHUNK, h_, t.shape[-1]).transpose(1, 0, 3, 2, 4)

    qc, gc, kc, vc = to_chunks(q), to_chunks(log_f), to_chunks(k), to_chunks(v)
    causal = jnp.asarray(np.tril(np.ones((HGRN_CHUNK, HGRN_CHUNK), dtype=bool)))

    def step(state, inp):
        qb, gb, kb, vb = inp
        cum = jnp.cumsum(gb, axis=2)
        o_inter = jnp.einsum('bhtk,bhkv->bhtv', qb * jnp.exp(cum), state)
        rel = cum[:, :, :, None, :] - cum[:, :, None, :, :]
        decay = jnp.exp(jnp.where(causal[:, :, None], rel, -jnp.inf))
        scores = jnp.einsum('bhtk,bhtsk,bhsk->bhts', qb, decay, kb)
        o_intra = jnp.einsum('bhts,bhsv->bhtv', scores, vb)
        last = cum[:, :, -1, :]
        new_state = jnp.exp(last)[..., None] * state + jnp.einsum(
            'bhsk,bhsv->bhkv', kb * jnp.exp(last[:, :, None, :] - cum), vb)
        return new_state, o_inter + o_intra

    state0 = jnp.zeros((b_, h_, dk, dv), jnp.float32)
    _, out = lax.scan(step, state0, (qc, gc, kc, vc))
    return out.transpose(1, 0, 3, 2, 4).reshape(b_, s_, h_, dv)


def hgrn2_mixer(xn, w_in, lb, out_norm, w_out):
    b_, s_, _ = xn.shape
    q, fz, inp, g = jnp.split(xn @ w_in, 4, axis=-1)
    fz32 = fz.astype(jnp.float32)
    log_f = jnp.logaddexp(jnp.log(lb), jnp.log1p(-lb) + jax.nn.log_sigmoid(fz32))
    k = (1.0 - lb) * jax.nn.sigmoid(-fz32)
    heads = lambda t: t.reshape(b_, s_, HGRN_HEADS, -1)
    o = hgrn2_chunked_scan(heads(q), heads(log_f), heads(k), heads(inp))
    o = rmsnorm(o, out_norm).astype(xn.dtype).reshape(b_, s_, HGRN_WIDTH)
    return (o * jax.nn.silu(g)) @ w_out


def shared_kv(h, kv_norm, w_kvf, b_f):
    b_, s_, _ = h.shape
    kvf = rmsnorm(h, kv_norm) @ w_kvf
    k = kvf[..., :FOX_WIDTH].reshape(b_, s_, FOX_HEADS, FOX_HEAD_DIM)
    v = kvf[..., FOX_WIDTH:2 * FOX_WIDTH].reshape(b_, s_, FOX_HEADS, FOX_HEAD_DIM)
    log_f = jax.nn.log_sigmoid((kvf[..., 2 * FOX_WIDTH:] + b_f).astype(jnp.float32))
    c = jnp.cumsum(log_f, axis=1).transpose(0, 2, 1)
    return k, v, c


def forgetting_attention(q, k, v, c):
    s_ = q.shape[1]
    scale = FOX_HEAD_DIM ** -0.5
    outs = []
    for blk in range(s_ // FOX_QBLOCK):
        t0 = blk * FOX_QBLOCK
        t1 = t0 + FOX_QBLOCK
        logits = jnp.einsum('bthd,bshd->bhts', q[:, t0:t1], k[:, :t1]).astype(jnp.float32) * scale
        logits = logits + c[:, :, t0:t1, None] - c[:, :, None, :t1]
        mask = jnp.asarray((t0 + np.arange(FOX_QBLOCK))[:, None] >= np.arange(t1)[None, :])
        logits = jnp.where(mask, logits, -jnp.inf)
        probs = jax.nn.softmax(logits, axis=-1).astype(v.dtype)
        outs.append(jnp.einsum('bhts,bshd->bthd', probs, v[:, :t1]))
    return jnp.concatenate(outs, axis=1)


def fox_mixer(xn, w_qg, w_out, k, v, c):
    b_, s_, _ = xn.shape
    q, g = jnp.split(xn @ w_qg, 2, axis=-1)
    q = q.reshape(b_, s_, FOX_HEADS, FOX_HEAD_DIM)
    o = forgetting_attention(q, k, v, c).reshape(b_, s_, FOX_WIDTH)
    return (o * jax.nn.sigmoid(g)) @ w_out


def setup_inputs(seed: int = 0) -> dict:
    key = jax.random.key(seed)
    ks = iter(jax.random.split(key, 32))
    nrm = lambda shape, scale: jax.random.normal(next(ks), shape, jnp.float32) * scale
    gain = lambda shape: 1.0 + nrm(shape, 0.1)
    return {
        "x": nrm((BATCH, SEQ, D_MODEL), 1.0),
        "p": nrm((DEPTH, BATCH, SEQ, PLE_DIM), 1.0),
        "ffn1_norm_pre": gain((DEPTH, D_MODEL)),
        "ffn1_w_in": nrm((DEPTH, D_MODEL, 2 * D_FF), D_MODEL ** -0.5),
        "ffn1_w_out": nrm((DEPTH, D_FF, D_MODEL), D_FF ** -0.5),
        "ffn1_norm_post": gain((DEPTH, D_MODEL)),
        "mix_norm_pre": gain((DEPTH, D_MODEL)),
        "mix_norm_post": gain((DEPTH, D_MODEL)),
        "ffn2_norm_pre": gain((DEPTH, D_MODEL)),
        "ffn2_w_in": nrm((DEPTH, D_MODEL, 2 * D_FF), D_MODEL ** -0.5),
        "ffn2_w_out": nrm((DEPTH, D_FF, D_MODEL), D_FF ** -0.5),
        "ffn2_norm_post": gain((DEPTH, D_MODEL)),
        "hgrn_w_in": nrm((N_A_LAYERS, D_MODEL, 4 * HGRN_WIDTH), D_MODEL ** -0.5),
        "hgrn_lb_logits": nrm((N_A_LAYERS + 1, HGRN_WIDTH), 0.5),
        "hgrn_out_norm": gain((N_A_LAYERS, HGRN_DV)),
        "hgrn_w_out": nrm((N_A_LAYERS, HGRN_WIDTH, D_MODEL), HGRN_WIDTH ** -0.5),
        "kv_norm": gain((D_MODEL,)),
        "fox_w_kvf": nrm((D_MODEL, 2 * FOX_WIDTH + FOX_HEADS), D_MODEL ** -0.5),
        "fox_b_f": FOX_GATE_BIAS_OFFSET + nrm((FOX_HEADS,), 0.1),
        "fox_w_qg": nrm((N_B_LAYERS, D_MODEL, 2 * FOX_WIDTH), D_MODEL ** -0.5),
        "fox_w_out": nrm((N_B_LAYERS, FOX_WIDTH, D_MODEL), FOX_WIDTH ** -0.5),
        "ple_norm_pre": gain((DEPTH, D_MODEL)),
        "ple_w_gate": nrm((DEPTH, D_MODEL, D_MODEL), D_MODEL ** -0.5),
        "ple_w_proj": nrm((DEPTH, PLE_DIM, D_MODEL), PLE_DIM ** -0.5),
        "ple_norm_post": gain((DEPTH, D_MODEL)),
    }


def reference(x, p, ffn1_norm_pre, ffn1_w_in, ffn1_w_out, ffn1_norm_post,
              mix_norm_pre, mix_norm_post,
              ffn2_norm_pre, ffn2_w_in, ffn2_w_out, ffn2_norm_post,
              hgrn_w_in, hgrn_lb_logits, hgrn_out_norm, hgrn_w_out,
              kv_norm, fox_w_kvf, fox_b_f, fox_w_qg, fox_w_out,
              ple_norm_pre, ple_w_gate, ple_w_proj, ple_norm_post):
    lb_all = jnp.cumsum(jax.nn.softmax(hgrn_lb_logits.astype(jnp.float32), axis=0), axis=0)
    h = x
    k_sh = v_sh = c_sh = None
    for i in range(DEPTH):
        h = h + 0.5 * rmsnorm(swiglu(rmsnorm(h, ffn1_norm_pre[i]), ffn1_w_in[i], ffn1_w_out[i]), ffn1_norm_post[i])
        hn = rmsnorm(h, mix_norm_pre[i])
        if i < N_A_LAYERS:
            mix = hgrn2_mixer(hn, hgrn_w_in[i], lb_all[i], hgrn_out_norm[i], hgrn_w_out[i])
        else:
            j = i - N_A_LAYERS
            mix = fox_mixer(hn, fox_w_qg[j], fox_w_out[j], k_sh, v_sh, c_sh)
        h = h + rmsnorm(mix, mix_norm_post[i])
        h = h + 0.5 * rmsnorm(swiglu(rmsnorm(h, ffn2_norm_pre[i]), ffn2_w_in[i], ffn2_w_out[i]), ffn2_norm_post[i])
        gate = jax.nn.sigmoid(rmsnorm(h, ple_norm_pre[i]) @ ple_w_gate[i])
        h = h + rmsnorm(gate * (p[i] @ ple_w_proj[i]), ple_norm_post[i])
        if i == N_A_LAYERS - 1:
            k_sh, v_sh, c_sh = shared_kv(h, kv_norm, fox_w_kvf, fox_b_f)
    return h
```

```python
import numpy as np
import concourse.bass as bass
import concourse.mybir as mybir
from concourse.bass_utils import run_bass_kernel_spmd

F32 = mybir.dt.float32
BF16 = mybir.dt.bfloat16
ALU = mybir.AluOpType
AF = mybir.ActivationFunctionType

S = 4096
D = 1024
DFF = 2816
PLE = 256
TT = 512
NT = S // TT
KD = D // 128
KF = DFF // 128
EPS = 1e-6
NCORES = 8


class Buf:
    __slots__ = ("name", "w", "r")

    def __init__(self, name=""):
        self.name = name
        self.w = None
        self.r = {}


class Tracker:
    ROLL = 30000

    def __init__(self, nc):
        self.nc = nc
        self.engs = dict(pe=nc.tensor, act=nc.scalar, dve=nc.vector, pool=nc.gpsimd, sp=nc.sync)
        self.semh = []
        self.psem = {}
        self.pcnt = {}
        for k in self.engs:
            self._newsem(k)
        self.waited = {}
        self.dq = {}
        for q, n in (("sp", 8), ("pool", 4), ("act", 2)):
            sems = [self._alloc(f"dq_{q}{i}") for i in range(n)]
            self.dq[q] = dict(sems=sems, cnt=[0] * n, nxt=0)
        self.ninst = 0

    def _alloc(self, name):
        self.semh.append(self.nc.alloc_semaphore(name))
        return len(self.semh) - 1

    def _newsem(self, k):
        self.psem[k] = self._alloc(f"ps_{k}_{len(self.semh)}")
        self.pcnt[k] = 0

    def _wait(self, ek, tok):
        if tok is None:
            return
        si, val = tok
        key = (ek, si)
        if self.waited.get(key, 0) >= val:
            return
        self.engs[ek].wait_ge(self.semh[si], val)
        self.waited[key] = val

    def _deps(self, ek, R, W):
        for b in R:
            if ek == "pe" and b.w is not None and b.w[0] == self.psem["pe"]:
                continue
            self._wait(ek, b.w)
        for b in W:
            if not (ek == "pe" and b.w is not None and b.w[0] == self.psem["pe"]):
                self._wait(ek, b.w)
            for si, val in b.r.items():
                if ek == "pe" and si == self.psem["pe"]:
                    continue
                self._wait(ek, (si, val))

    def _commit(self, tok, R, W):
        si, val = tok
        for b in R:
            if b.r.get(si, 0) < val:
                b.r[si] = val
        for b in W:
            b.w = tok
            b.r = {}

    def _inc(self, ek, inst):
        if self.pcnt[ek] >= self.ROLL:
            self._newsem(ek)
        self.pcnt[ek] += 1
        inst.then_inc(self.semh[self.psem[ek]], 1)
        return (self.psem[ek], self.pcnt[ek])

    def op(self, ek, fn, R=(), W=()):
        self._deps(ek, R, W)
        inst = fn(self.engs[ek])
        tok = self._inc(ek, inst)
        self._commit(tok, R, W)
        self.ninst += 1

    def group(self, ek, fns, R=(), W=()):
        self._deps(ek, R, W)
        inst = None
        for fn in fns:
            inst = fn(self.engs[ek])
            self.ninst += 1
        tok = self._inc(ek, inst)
        self._commit(tok, R, W)

    def dma(self, q, out, in_, R=(), W=(), **kw):
        self._deps(q, R, W)
        dq = self.dq[q]
        i = dq["nxt"]
        dq["nxt"] = (i + 1) % len(dq["sems"])
        si = dq["sems"][i]
        if dq["cnt"][i] > 0:
            self._wait(q, (si, dq["cnt"][i]))
        inst = self.engs[q].dma_start(out=out, in_=in_, **kw)
        dq["cnt"][i] += 16
        inst.then_inc(self.semh[si], 16)
        self._commit((si, dq["cnt"][i]), R, W)
        self.ninst += 1

    def barrier(self):
        toks = [(self.psem[k], self.pcnt[k]) for k in self.engs if self.pcnt[k] > 0]
        for q in self.dq.values():
            for si, c in zip(q["sems"], q["cnt"]):
                if c > 0:
                    toks.append((si, c))
        for ek in self.engs:
            for tok in toks:
                if tok[0] == self.psem[ek]:
                    continue
                self._wait(ek, tok)


class SB:
    def __init__(self, t, name, nbuf=1):
        self.t = t
        self.b = Buf(name)
        self.bs = [Buf(f"{name}{i}") for i in range(nbuf)] if nbuf > 1 else [self.b]


GAIN_NAMES = ["ffn1_norm_pre", "ffn1_norm_post", "mix_norm_pre", "mix_norm_post",
              "ffn2_norm_pre", "ffn2_norm_post", "ple_norm_pre", "ple_norm_post"]


def gidx(name, layer):
    return (GAIN_NAMES.index(name) * 2 + layer) * KD


G_KV = 16 * KD
G_LB0 = 17 * KD
G_LB1 = 18 * KD
G_ON = 19 * KD
G_TOT = 19 * KD + 1


class Prog:
    def __init__(self, cfg):
        self.cfg = cfg
        nc = bass.Bass("TRN2", target_bir_lowering=False)
        self.nc = nc
        self.T = Tracker(nc)
        dt = nc.dram_tensor
        self.x = dt("x", [S, D], F32, kind="ExternalInput").ap()
        self.p = dt("p", [2, S, PLE], F32, kind="ExternalInput").ap()
        self.ffn_w_in = [dt(f"ffn{j + 1}_w_in", [2, D, 2 * DFF], F32, kind="ExternalInput").ap() for j in range(2)]
        self.ffn_w_out = [dt(f"ffn{j + 1}_w_out", [2, DFF, D], F32, kind="ExternalInput").ap() for j in range(2)]
        self.hgrn_w_in = dt("hgrn_w_in", [1, D, 4 * D], F32, kind="ExternalInput").ap()
        self.hgrn_w_out = dt("hgrn_w_out", [1, D, D], F32, kind="ExternalInput").ap()
        self.fox_w_kvf = dt("fox_w_kvf", [D, 2 * D + 16], F32, kind="ExternalInput").ap()
        self.fox_w_qg = dt("fox_w_qg", [1, D, 2 * D], F32, kind="ExternalInput").ap()
        self.fox_w_out = dt("fox_w_out", [1, D, D], F32, kind="ExternalInput").ap()
        self.ple_w_gate = dt("ple_w_gate", [2, D, D], F32, kind="ExternalInput").ap()
        self.ple_w_proj = dt("ple_w_proj", [2, PLE, D], F32, kind="ExternalInput").ap()
        self.gains = dt("gains", [128, G_TOT], F32, kind="ExternalInput").ap()
        self.bf_bc = dt("bf_bc", [128, 16], F32, kind="ExternalInput").ap()
        self.ident_d = dt("ident", [128, 128], F32, kind="ExternalInput").ap()
        self.out = dt("out", [S, D], F32, kind="ExternalOutput").ap()
        self.hbuf = dt("hbuf", [KD, 128, S], F32, kind="Internal").ap()
        self.hb = [Buf(f"hb{i}") for i in range(NT)]
        dbg = "ExternalOutput"
        self.KTd = dt("KTd", [8, 128, S], BF16, kind=dbg).ap()
        self.Vd = dt("Vd", [8, 128, S // 128, 192], BF16, kind=dbg).ap()
        self.CLd = dt("CLd", [S, 16], F32, kind=dbg).ap()
        self.ktb = [Buf(f"ktb{i}") for i in range(NT)]
        self.vdb = [Buf(f"vdb{i}") for i in range(NT)]
        self.cldb = [Buf(f"cldb{i}") for i in range(NT)]
        self.cmat = dt("cmat", [128, 4, 128], F32, kind="ExternalInput").ap()
        self.rmask = dt("rmask", [128, TT + 128], F32, kind="ExternalInput").ap()
        if cfg.get("x2") is not None:
            self.x2 = dt("x2", [S, D], F32, kind="ExternalInput").ap()

        a = nc.alloc_sbuf_tensor
        self.WBIG = a("WBIG", [128, 66 * 1024], BF16)
        self.wb = Buf("wbig")
        self.HY = SB(a("HY", [128, KD, TT], F32), "HY")
        self.XNT = SB(a("XNT", [128, KD, TT], BF16), "XNT")
        self.SQ = SB(a("SQ", [128, 2, TT], BF16), "SQ", 2)
        self.HID = SB(a("HID", [128, KF, TT], BF16), "HID", KF)
        self.SG = SB(a("SG", [128, 2, TT], F32), "SG", 2)
        self.RS = SB(a("RS", [128, 2, TT], F32), "RS", 2)
        self.GA = SB(a("GA", [128, G_TOT], F32), "GA")
        self.GH = SB(a("GH", [128, 4 * KD], F32), "GH")
        self.ONES = SB(a("ONES", [128, 128], BF16), "ONES")
        self.IDF = SB(a("IDF", [128, 128], F32), "IDF")
        self.XIN = SB(a("XIN", [128, 2, D], F32), "XIN", 2)
        self.CM = SB(a("CM", [128, 4, 128], F32), "CM")
        self.CB = SB(a("CB", [128, 2, 128], BF16), "CB")
        self.CLS = SB(a("CLS", [128, S // 128, 16], F32), "CLS")
        self.CREF = SB(a("CREF", [128, NT, 16], F32), "CREF")
        self.BIAS = SB(a("BIAS", [128, 2, 32], F32), "BIAS", 2)
        self.FG = SB(a("FG", [128, 64], F32), "FG")
        self.CAR = SB(a("CAR", [128, 16], F32), "CAR")
        self.CLST = SB(a("CLST", [128, 4, 16], F32), "CLST")
        self.BFB = SB(a("BFB", [128, 16], F32), "BFB")
        self.RMK = SB(a("RMK", [128, TT], F32), "RMK")
        self.BDM = SB(a("BDM", [128, 128], F32), "BDM")
        self.PS = [SB(nc.alloc_psum_tensor(f"ps{i}", [128, TT], F32), f"ps{i}") for i in range(8)]

    def setup(self):
        T = self.T
        T.dma("sp", self.GA.t[:, :], self.gains[:, :], W=[self.GA.b])
        T.dma("sp", self.IDF.t[:, :], self.ident_d[:, :], W=[self.IDF.b])
        T.op("dve", lambda e: e.memset(self.ONES.t[:, :], 1.0), W=[self.ONES.b])
        T.dma("sp", self.CM.t[:, :, :], self.cmat[:, :, :], W=[self.CM.b])
        T.dma("sp", self.BFB.t[:, :], self.bf_bc[:, :], W=[self.BFB.b])
        T.dma("sp", self.RMK.t[:, :], self.rmask[:, 0:TT], W=[self.RMK.b])
        T.dma("sp", self.BDM.t[:, :], self.rmask[:, TT:TT + 128], W=[self.BDM.b])
        T.op("dve", lambda e: e.tensor_copy(out=self.CB.t[:, 0, :], in_=self.IDF.t[:, :]), R=[self.IDF.b], W=[self.CB.b])
        T.op("dve", lambda e: e.tensor_copy(out=self.CB.t[:, 1, :], in_=self.CM.t[:, 3, :]), R=[self.CM.b], W=[self.CB.b])
        for j, nm in enumerate(["ffn1_norm_post", "ffn2_norm_post"]):
            for layer in range(2):
                o = (j * 2 + layer) * KD
                g = gidx(nm, layer)
                T.op("dve", lambda e, o=o, g=g: e.tensor_scalar(
                    out=self.GH.t[:, o:o + KD], in0=self.GA.t[:, g:g + KD], scalar1=0.5, scalar2=None,
                    op0=ALU.mult), R=[self.GA.b], W=[self.GH.b])

    def hb_tile(self, it):
        return self.hbuf[:, :, it * TT:(it + 1) * TT].rearrange("k p t -> p k t")

    def load_w(self, dst_off, src_ap, nk, ncols, nsplit=1):
        T = self.T
        step = ncols // nsplit
        dst = self.WBIG[:, dst_off:dst_off + nk * ncols].rearrange("p (k n) -> p k n", k=nk)
        src = src_ap.rearrange("(k p) n -> p k n", p=128)
        for i in range(nsplit):
            T.dma("pool", dst[:, :, i * step:(i + 1) * step], src[:, :, i * step:(i + 1) * step], W=[self.wb])

    def rstd_from_sq(self, ps, rs_slot):
        T = self.T
        rs = self.RS.t[:, rs_slot, :]
        rb = self.RS.bs[rs_slot]
        T.op("act", lambda e: e.activation(out=rs, in_=ps.t[:, :], func=AF.Sqrt, scale=1.0 / D, bias=EPS),
             R=[ps.b], W=[rb])
        T.op("dve", lambda e: e.reciprocal(out=rs, in_=rs), R=[rb], W=[rb])

    def prenorm(self, gcol, ps_stats):
        T = self.T
        for k in range(KD):
            sl = k % 2
            T.op("act", lambda e, k=k, sl=sl: e.activation(out=self.SQ.t[:, sl, :], in_=self.HY.t[:, k, :], func=AF.Square),
                 R=[self.HY.b], W=[self.SQ.bs[sl]])
            T.op("pe", lambda e, k=k, sl=sl: e.matmul(ps_stats.t[:, :], lhsT=self.ONES.t[:, :], rhs=self.SQ.t[:, sl, :],
                                                      start=(k == 0), stop=(k == KD - 1)),
                 R=[self.ONES.b, self.SQ.bs[sl]], W=[ps_stats.b])
        self.rstd_from_sq(ps_stats, 0)
        for k in range(KD):
            T.op("dve", lambda e, k=k: e.scalar_tensor_tensor(
                out=self.XNT.t[:, k, :], in0=self.HY.t[:, k, :], scalar=self.GA.t[:, gcol + k:gcol + k + 1],
                in1=self.RS.t[:, 0, :], op0=ALU.mult, op1=ALU.mult),
                R=[self.HY.b, self.GA.b, self.RS.bs[0]], W=[self.XNT.b])

    def postnorm_store(self, it, gsb, gcol, ps_stats):
        T = self.T
        for k in range(KD):
            sl = k % 2
            T.op("act", lambda e, k=k, sl=sl: e.activation(out=self.SQ.t[:, sl, :], in_=self.HY.t[:, k, :], func=AF.Square),
                 R=[self.HY.b], W=[self.SQ.bs[sl]])
            T.op("pe", lambda e, k=k, sl=sl: e.matmul(ps_stats.t[:, :], lhsT=self.ONES.t[:, :], rhs=self.SQ.t[:, sl, :],
                                                      start=(k == 0), stop=(k == KD - 1)),
                 R=[self.ONES.b, self.SQ.bs[sl]], W=[ps_stats.b])
        self.rstd_from_sq(ps_stats, 1)
        for k in range(KD):
            T.op("dve", lambda e, k=k: e.scalar_tensor_tensor(
                out=self.HY.t[:, k, :], in0=self.HY.t[:, k, :], scalar=gsb.t[:, gcol + k:gcol + k + 1],
                in1=self.RS.t[:, 1, :], op0=ALU.mult, op1=ALU.mult),
                R=[self.HY.b, gsb.b, self.RS.bs[1]], W=[self.HY.b])
        T.dma("pool", self.hb_tile(it), self.HY.t[:, :, :], R=[self.HY.b], W=[self.hb[it]], accum_op=ALU.add)

    def phase_in(self, xsrc=None):
        T = self.T
        xsrc = self.x if xsrc is None else xsrc
        for it in range(NT):
            for s in range(4):
                t0 = it * TT + s * 128
                sl = s % 2
                T.dma("sp", self.XIN.t[:, sl, :], xsrc[t0:t0 + 128, :], W=[self.XIN.bs[sl]])
                for half in range(2):
                    ps = self.PS[half]
                    T.group("pe", [lambda e, j=j, half=half, sl=sl, ps=ps: e.transpose(
                        out=ps.t[:, j * 128:(j + 1) * 128],
                        in_=self.XIN.t[:, sl, (half * 4 + j) * 128:(half * 4 + j + 1) * 128],
                        identity=self.IDF.t[:, :]) for j in range(4)],
                        R=[self.XIN.bs[sl], self.IDF.b], W=[ps.b])
                    dst = self.HY.t[:, half * 4:half * 4 + 4, s * 128:(s + 1) * 128]
                    src = ps.t[:, :].rearrange("p (j t) -> p j t", j=4)
                    if half == 0:
                        T.op("act", lambda e, dst=dst, src=src: e.copy(out=dst, in_=src), R=[ps.b], W=[self.HY.b])
                    else:
                        T.op("dve", lambda e, dst=dst, src=src: e.tensor_copy(out=dst, in_=src), R=[ps.b], W=[self.HY.b])
            T.dma("sp", self.hb_tile(it), self.HY.t[:, :, :], R=[self.HY.b], W=[self.hb[it]])

    def wview(self, off, nk, ncols):
        return self.WBIG[:, off:off + nk * ncols].rearrange("p (k n) -> p k n", k=nk)

    def load_h(self, it):
        self.T.dma("sp", self.HY.t[:, :, :], self.hb_tile(it), R=[self.hb[it]], W=[self.HY.b])

    def evac(self, i, out, in_, R, W):
        if i % 2 == 0:
            self.T.op("act", lambda e: e.copy(out=out, in_=in_), R=R, W=W)
        else:
            self.T.op("dve", lambda e: e.tensor_copy(out=out, in_=in_), R=R, W=W)

    def phase_ple(self, layer):
        T = self.T
        WG = 0
        WP = KD * D
        self.load_w(WG, self.ple_w_gate[layer], KD, D, nsplit=2)
        self.load_w(WP, self.ple_w_proj[layer], 2, D)
        wg = self.wview(WG, KD, D)
        wp = self.wview(WP, 2, D)
        gpre = gidx("ple_norm_pre", layer)
        gpost = gidx("ple_norm_post", layer)
        ptb = Buf("ptb")
        for it in range(NT):
            self.load_h(it)
            self.prenorm(gpre, self.PS[0])
            for s in range(4):
                t0 = it * TT + s * 128
                sl = s % 2
                T.dma("sp", self.XIN.t[:, sl, 0:PLE], self.p[layer, t0:t0 + 128, :], W=[self.XIN.bs[sl]])
                ps = self.PS[1]
                T.group("pe", [lambda e, j=j, sl=sl, ps=ps: e.transpose(
                    out=ps.t[:, j * 128:(j + 1) * 128], in_=self.XIN.t[:, sl, j * 128:(j + 1) * 128],
                    identity=self.IDF.t[:, :]) for j in range(2)],
                    R=[self.XIN.bs[sl], self.IDF.b], W=[ps.b])
                dst = self.HID.t[:, 0:2, s * 128:(s + 1) * 128]
                srcv = ps.t[:, 0:256].rearrange("p (j t) -> p j t", j=2)
                T.op("act", lambda e, dst=dst, srcv=srcv: e.copy(out=dst, in_=srcv), R=[ps.b], W=[ptb])
            for m in range(KD):
                pg = self.PS[2 + (m % 2)]
                pu = self.PS[4 + (m % 2)]
                sl = m % 2
                T.group("pe", [lambda e, k=k, m=m, pg=pg: e.matmul(
                    pg.t[:, :], lhsT=wg[:, k, m * 128:(m + 1) * 128], rhs=self.XNT.t[:, k, :],
                    start=(k == 0), stop=(k == KD - 1)) for k in range(KD)],
                    R=[self.wb, self.XNT.b], W=[pg.b])
                T.group("pe", [lambda e, k=k, m=m, pu=pu: e.matmul(
                    pu.t[:, :], lhsT=wp[:, k, m * 128:(m + 1) * 128], rhs=self.HID.t[:, k, :],
                    start=(k == 0), stop=(k == 1)) for k in range(2)],
                    R=[self.wb, ptb], W=[pu.b])
                T.op("act", lambda e, sl=sl, pg=pg: e.activation(out=self.SG.t[:, sl, :], in_=pg.t[:, :], func=AF.Sigmoid),
                     R=[pg.b], W=[self.SG.bs[sl]])
                T.op("dve", lambda e, sl=sl, pu=pu, m=m: e.tensor_tensor(
                    out=self.HY.t[:, m, :], in0=self.SG.t[:, sl, :], in1=pu.t[:, :], op=ALU.mult),
                    R=[self.SG.bs[sl], pu.b], W=[self.HY.b])
            self.postnorm_store(it, self.GA, gpost, self.PS[7])

    def phase_kvf(self):
        T = self.T
        NW = 2 * D + 16
        self.load_w(0, self.fox_w_kvf, KD, NW, nsplit=2)
        wk = self.wview(0, KD, NW)
        kstb = Buf("kstb")
        vstb = Buf("vstb")
        VSTf = self.HID.t[:, 8:20, :].rearrange("p c t -> p (c t)")
        VST = VSTf.rearrange("p (s h c) -> p s h c", s=4, h=8)
        T.op("dve", lambda e: e.memset(VSTf, 0.0), W=[vstb])
        T.op("dve", lambda e: e.memset(VST[:, :, :, 64:65], 1.0), W=[vstb])
        T.op("dve", lambda e: e.memset(self.CAR.t[:, :], 0.0), W=[self.CAR.b])
        UT = self.CM.t[:, 0, :]
        ONESF = self.CM.t[:, 1, :]
        for it in range(NT):
            self.load_h(it)
            self.prenorm(G_KV, self.PS[0])
            for hp in range(8):
                pk = self.PS[1 + (hp % 2)]
                T.group("pe", [lambda e, k=k, hp=hp, pk=pk: e.matmul(
                    pk.t[:, :], lhsT=wk[:, k, hp * 128:(hp + 1) * 128], rhs=self.XNT.t[:, k, :],
                    start=(k == 0), stop=(k == KD - 1)) for k in range(KD)],
                    R=[self.wb, self.XNT.b], W=[pk.b])
                self.evac(hp, self.HID.t[:, hp, :], pk.t[:, :], R=[pk.b], W=[kstb])
            T.dma("sp", self.KTd[:, :, it * TT:(it + 1) * TT].rearrange("h p t -> p h t"), self.HID.t[:, 0:8, :],
                  R=[kstb], W=[self.ktb[it]])
            for s in range(4):
                for c in range(2):
                    pv = self.PS[3 + c]
                    T.group("pe", [lambda e, k=k, s=s, c=c, pv=pv: e.matmul(
                        pv.t[:, :], lhsT=self.XNT.t[:, k, s * 128:(s + 1) * 128],
                        rhs=wk[:, k, D + c * 512:D + (c + 1) * 512],
                        start=(k == 0), stop=(k == KD - 1)) for k in range(KD)],
                        R=[self.wb, self.XNT.b], W=[pv.b])
                    srcv = pv.t[:, :].rearrange("p (h e d) -> p h e d", h=4, e=2)
                    self.evac(0, VST[:, s, 4 * c:4 * c + 4, 0:64], srcv[:, :, 0, :], R=[pv.b], W=[vstb])
                    self.evac(1, VST[:, s, 4 * c:4 * c + 4, 128:192], srcv[:, :, 1, :], R=[pv.b], W=[vstb])
                pf = self.PS[5]
                T.group("pe", [lambda e, k=k, s=s, pf=pf: e.matmul(
                    pf.t[:, 0:16], lhsT=self.XNT.t[:, k, s * 128:(s + 1) * 128], rhs=wk[:, k, 2 * D:2 * D + 16],
                    start=(k == 0), stop=(k == KD - 1)) for k in range(KD)],
                    R=[self.wb, self.XNT.b], W=[pf.b])
                fg = self.FG
                T.op("dve", lambda e, pf=pf: e.tensor_tensor(out=fg.t[:, 0:16], in0=pf.t[:, 0:16], in1=self.BFB.t[:, :], op=ALU.add),
                     R=[pf.b, self.BFB.b], W=[fg.b])
                T.op("act", lambda e: e.activation(out=fg.t[:, 16:32], in_=fg.t[:, 0:16], func=AF.Exp, scale=-1.0),
                     R=[fg.b], W=[fg.b])
                T.op("act", lambda e: e.activation(out=fg.t[:, 32:48], in_=fg.t[:, 16:32], func=AF.Ln, bias=1.0),
                     R=[fg.b], W=[fg.b])
                pc = self.PS[6]
                pt = self.PS[7]
                T.op("pe", lambda e, pc=pc: e.matmul(pc.t[:, 0:16], lhsT=UT, rhs=fg.t[:, 32:48], start=True, stop=True),
                     R=[self.CM.b, fg.b], W=[pc.b])
                T.op("pe", lambda e, pt=pt: e.matmul(pt.t[:, 0:16], lhsT=ONESF, rhs=fg.t[:, 32:48], start=True, stop=True),
                     R=[self.CM.b, fg.b], W=[pt.b])
                T.op("dve", lambda e, s=s, pc=pc: e.tensor_tensor(out=self.CLST.t[:, s, :], in0=pc.t[:, 0:16], in1=self.CAR.t[:, :], op=ALU.add),
                     R=[pc.b, self.CAR.b], W=[self.CLST.b])
                T.op("dve", lambda e, pt=pt: e.tensor_tensor(out=self.CAR.t[:, :], in0=pt.t[:, 0:16], in1=self.CAR.t[:, :], op=ALU.add),
                     R=[pt.b, self.CAR.b], W=[self.CAR.b])
            for hp in range(8):
                T.dma("sp", self.Vd[hp, :, it * 4:(it + 1) * 4, :], VST[:, :, hp, :],
                      R=[vstb], W=[self.vdb[it]])
            T.dma("sp", self.CLd[it * TT:(it + 1) * TT, :].rearrange("(s p) h -> p s h", p=128), self.CLST.t[:, :, :],
                  R=[self.CLST.b], W=[self.cldb[it]])

    def phase_fox(self):
        T = self.T
        WQ = 0
        WO = KD * 2 * D
        self.load_w(WQ, self.fox_w_qg[0], KD, 2 * D, nsplit=2)
        self.load_w(WO, self.fox_w_out[0], KD, D, nsplit=1)
        wq = self.wview(WQ, KD, 2 * D)
        wo = self.wview(WO, KD, D)
        KVOFF = WO + KD * D
        KSZ = S
        VSZ = (S // 128) * 192
        kslots = []
        vslots = []
        for i in range(2):
            o = KVOFF + i * (KSZ + VSZ)
            kslots.append(self.WBIG[:, o:o + KSZ])
            vslots.append(self.WBIG[:, o + KSZ:o + KSZ + VSZ].rearrange("p (n c) -> p n c", c=192))
        kvb = [Buf("kv0"), Buf("kv1")]
        gpre = gidx("mix_norm_pre", 1)
        gpost = gidx("mix_norm_post", 1)
        qtb = Buf("qtb")
        ogb = Buf("ogb")
        ptb = [Buf("pte0"), Buf("pte1")]
        IDB = self.CB.t[:, 0, :]
        MASKB = self.CB.t[:, 1, :]
        SEL2 = self.CM.t[:, 2, :]
        LSB = self.XIN.t[:, 0, 0:512]
        RINV = self.XIN.t[:, 0, 512:1024]
        lsb = self.XIN.bs[0]
        T.dma("sp", self.CLS.t[:, :, :], self.CLd.rearrange("(n p) h -> p n h", p=128), R=self.cldb, W=[self.CLS.b])
        for qt in range(NT):
            T.dma("sp", self.CREF.t[:, qt, :], self.CLd[qt * TT, :].partition_broadcast(128), R=self.cldb, W=[self.CREF.b])
        T.op("dve", lambda e: e.memset(self.XIN.t[:, 0, :], 0.0), W=[lsb])
        slot_ctr = 0
        for qt in range(NT):
            nk = 4 * (qt + 1)
            self.load_h(qt)
            self.prenorm(gpre, self.PS[0])
            for hp in range(8):
                pq = self.PS[1 + 6 * (hp % 2)]
                T.group("pe", [lambda e, k=k, hp=hp, pq=pq: e.matmul(
                    pq.t[:, :], lhsT=wq[:, k, hp * 128:(hp + 1) * 128], rhs=self.XNT.t[:, k, :],
                    start=(k == 0), stop=(k == KD - 1)) for k in range(KD)],
                    R=[self.wb, self.XNT.b], W=[pq.b])
                T.op("act", lambda e, hp=hp, pq=pq: e.activation(out=self.HID.t[:, hp, :], in_=pq.t[:, :], func=AF.Copy, scale=0.125),
                     R=[pq.b], W=[qtb])
            for hp in range(8):
                sl = slot_ctr % 2
                slot_ctr += 1
                T.dma("sp", kslots[sl][:, 0:nk * 128], self.KTd[hp, :, 0:nk * 128], R=self.ktb[:qt + 1], W=[kvb[sl]])
                T.dma("sp", vslots[sl][:, 0:nk, :], self.Vd[hp, :, 0:nk, :], R=self.vdb[:qt + 1], W=[kvb[sl]])
                po = [self.PS[4], self.PS[5]]
                for e_ in range(2):
                    h = hp * 2 + e_
                    bsl = h % 2
                    T.op("dve", lambda e, bsl=bsl, h=h, qt=qt, nk=nk: e.tensor_scalar(
                        out=self.BIAS.t[:, bsl, 0:nk], in0=self.CLS.t[:, 0:nk, h], scalar1=self.CREF.t[:, qt, h:h + 1],
                        scalar2=None, op0=ALU.subtract),
                        R=[self.CLS.b, self.CREF.b], W=[self.BIAS.bs[bsl]])
                    pr = slice(e_ * 64, (e_ + 1) * 64)
                    vcols = slice(0, 65) if e_ == 0 else slice(64, 192)
                    for kt in range(nk):
                        j = kt - 4 * qt
                        c0 = 128 * j if j > 0 else 0
                        ps_s = self.PS[2 + (kt % 2)]
                        fns = [lambda e, kt=kt, c0=c0, ps_s=ps_s, sl=sl, pr=pr, hp=hp, j=j: e.matmul(
                            ps_s.t[:, c0:TT], lhsT=kslots[sl][pr, kt * 128:(kt + 1) * 128],
                            rhs=self.HID.t[pr, hp, c0:TT], start=True, stop=(j < 0))]
                        if j >= 0:
                            fns.append(lambda e, c0=c0, ps_s=ps_s: e.matmul(
                                ps_s.t[:, c0:c0 + 128], lhsT=IDB, rhs=MASKB, start=False, stop=True))
                        T.group("pe", fns, R=[kvb[sl], qtb, self.CB.b], W=[ps_s.b])
                        pte = self.HID.t[:, 16 + (kt % 2), :]
                        T.op("act", lambda e, pte=pte, ps_s=ps_s, c0=c0, bsl=bsl, kt=kt: e.activation(
                            out=pte[:, c0:TT], in_=ps_s.t[:, c0:TT], func=AF.Exp, bias=self.BIAS.t[:, bsl, kt:kt + 1], scale=1.0),
                            R=[ps_s.b, self.BIAS.bs[bsl]], W=[ptb[kt % 2]])
                        T.op("pe", lambda e, pte=pte, c0=c0, kt=kt, sl=sl, vcols=vcols, e_=e_, nk=nk: e.matmul(
                            po[e_].t[0:(65 if e_ == 0 else 128), c0:TT], lhsT=vslots[sl][:, kt, vcols], rhs=pte[:, c0:TT],
                            start=(kt == 0), stop=(kt == nk - 1)),
                            R=[kvb[sl], ptb[kt % 2]], W=[po[e_].b])
                T.op("act", lambda e: e.copy(out=LSB[64:65, :], in_=po[0].t[64:65, :]), R=[po[0].b], W=[lsb])
                T.op("act", lambda e: e.copy(out=LSB[0:1, :], in_=po[1].t[0:1, :]), R=[po[1].b], W=[lsb])
                pl = self.PS[6]
                T.op("pe", lambda e, pl=pl: e.matmul(pl.t[:, :], lhsT=SEL2, rhs=LSB, start=True, stop=True),
                     R=[self.CM.b, lsb], W=[pl.b])
                T.op("dve", lambda e, pl=pl: e.reciprocal(out=RINV, in_=pl.t[:, :]), R=[pl.b], W=[lsb])
                pg = self.PS[1]
                T.group("pe", [lambda e, k=k, hp=hp, pg=pg: e.matmul(
                    pg.t[:, :], lhsT=wq[:, k, D + hp * 128:D + (hp + 1) * 128], rhs=self.XNT.t[:, k, :],
                    start=(k == 0), stop=(k == KD - 1)) for k in range(KD)],
                    R=[self.wb, self.XNT.b], W=[pg.b])
                T.op("act", lambda e, pg=pg: e.activation(out=self.SG.t[:, 0, :], in_=pg.t[:, :], func=AF.Sigmoid),
                     R=[pg.b], W=[self.SG.bs[0]])
                T.op("dve", lambda e: e.tensor_tensor(out=self.SG.t[0:64, 1, :], in0=po[0].t[0:64, :], in1=RINV[0:64, :], op=ALU.mult),
                     R=[po[0].b, lsb], W=[self.SG.bs[1]])
                T.op("dve", lambda e: e.tensor_tensor(out=self.SG.t[64:128, 1, :], in0=po[1].t[64:128, :], in1=RINV[64:128, :], op=ALU.mult),
                     R=[po[1].b, lsb], W=[self.SG.bs[1]])
                T.op("dve", lambda e, hp=hp: e.tensor_tensor(out=self.HID.t[:, 8 + hp, :], in0=self.SG.t[:, 1, :], in1=self.SG.t[:, 0, :], op=ALU.mult),
                     R=self.SG.bs, W=[ogb])
            for m in range(KD):
                py = self.PS[1 + 6 * (m % 2)]
                T.group("pe", [lambda e, k=k, m=m, py=py: e.matmul(
                    py.t[:, :], lhsT=wo[:, k, m * 128:(m + 1) * 128], rhs=self.HID.t[:, 8 + k, :],
                    start=(k == 0), stop=(k == KD - 1)) for k in range(KD)],
                    R=[self.wb, ogb], W=[py.b])
                self.evac(m, self.HY.t[:, m, :], py.t[:, :], R=[py.b], W=[self.HY.b])
            self.postnorm_store(qt, self.GA, gpost, self.PS[0])

    def phase_hgrn(self):
        T = self.T
        WI = 0
        WO = KD * 4 * D
        self.load_w(WI, self.hgrn_w_in[0], KD, 4 * D, nsplit=8)
        self.load_w(WO, self.hgrn_w_out[0], KD, D, nsplit=1)
        win = self.wview(WI, KD, 4 * D)
        wout = self.wview(WO, KD, D)
        FREE = WO + KD * D
        o = FREE
        SF = self.WBIG[:, o:o + 2048].bitcast(F32).rearrange("p (h v) -> p h v", h=8); o += 2048
        OSB = self.WBIG[:, o:o + 8192].bitcast(F32).rearrange("p (h t) -> p h t", h=8); o += 8192
        QTL = self.WBIG[:, o:o + 4096].rearrange("p (h t) -> p h t", h=8); o += 4096
        KTL = self.WBIG[:, o:o + 4096].rearrange("p (h t) -> p h t", h=8); o += 4096
        KHT = self.WBIG[:, o:o + 8192].bitcast(F32).rearrange("p (h t) -> p h t", h=8); o += 8192
        assert o <= 66 * 1024
        VSB = self.HID.t[:, 0:8, :].rearrange("p c t -> p (c t)").rearrange("p (s f) -> p s f", s=4)
        OG = self.HID.t[:, 8:16, :]
        ATL = self.HID.t[:, 16:18, :].rearrange("p c t -> p (c t)").rearrange("p (h t) -> p h t", h=8)
        KHS = self.HID.t[:, 18:20, :].rearrange("p c t -> p (c t)").rearrange("p (h k) -> p h k", h=8)
        SBF = self.HID.t[:, 20:22, :].rearrange("p c t -> p (c t)").rearrange("p (h v) -> p h v", h=8)
        A = [self.XIN.t[:, i // 2, (i % 2) * 512:(i % 2 + 1) * 512] for i in range(4)]
        ab = [Buf(f"A{i}") for i in range(4)]
        sfb = [Buf(f"sf{h}") for h in range(8)]
        sbfb = [Buf(f"sbf{h}") for h in range(8)]
        osb = [Buf(f"osb{h}") for h in range(8)]
        qtb = [Buf(f"qt{h}") for h in range(8)]
        ktb = [Buf(f"kt{h}") for h in range(8)]
        khtb = [Buf(f"kht{h}") for h in range(8)]
        atb = [Buf(f"at{h}") for h in range(8)]
        khsb = [Buf(f"khs{h}") for h in range(8)]
        ogb = Buf("ogb")
        vsb = Buf("vsb")
        elb = Buf("elb")
        LBT = self.FG
        ELAST = self.CLS.t[:, 0:4, :].rearrange("p a b -> p (a b)").rearrange("p (h c) -> p h c", h=8)
        RM = self.RMK.t[:, :]
        BDM = self.CM.t[:, 3, :] if False else self.BDM.t[:, :]
        gpre = gidx("mix_norm_pre", 0)
        gpost = gidx("mix_norm_post", 0)
        T.op("dve", lambda e: e.tensor_tensor(out=LBT.t[:, 16:24], in0=self.GA.t[:, G_LB0:G_LB0 + 8], in1=self.GA.t[:, G_LB1:G_LB1 + 8], op=ALU.subtract),
             R=[self.GA.b], W=[LBT.b])
        T.op("act", lambda e: e.activation(out=LBT.t[:, 0:8], in_=LBT.t[:, 16:24], func=AF.Sigmoid), R=[LBT.b], W=[LBT.b])
        T.op("dve", lambda e: e.tensor_scalar(out=LBT.t[:, 8:16], in0=LBT.t[:, 0:8], scalar1=-1.0, scalar2=1.0, op0=ALU.mult, op1=ALU.add),
             R=[LBT.b], W=[LBT.b])
        T.op("dve", lambda e: e.memset(SF, 0.0), W=sfb)
        T.op("dve", lambda e: e.memset(SBF, 0.0), W=sbfb)
        for it in range(NT):
            self.load_h(it)
            self.prenorm(gpre, self.PS[0])
            for s in range(4):
                for c in range(2):
                    pv = self.PS[1 + c]
                    T.group("pe", [lambda e, k=k, s=s, c=c, pv=pv: e.matmul(
                        pv.t[:, :], lhsT=self.XNT.t[:, k, s * 128:(s + 1) * 128],
                        rhs=win[:, k, 2 * D + c * 512:2 * D + (c + 1) * 512],
                        start=(k == 0), stop=(k == KD - 1)) for k in range(KD)],
                        R=[self.wb, self.XNT.b], W=[pv.b])
                    self.evac(c, VSB[:, s, c * 512:(c + 1) * 512], pv.t[:, :], R=[pv.b], W=[vsb])
            for hd in range(8):
                pq = self.PS[1]
                pz = self.PS[2]
                T.group("pe", [lambda e, k=k, hd=hd: e.matmul(
                    pq.t[:, :], lhsT=win[:, k, hd * 128:(hd + 1) * 128], rhs=self.XNT.t[:, k, :],
                    start=(k == 0), stop=(k == KD - 1)) for k in range(KD)],
                    R=[self.wb, self.XNT.b], W=[pq.b])
                T.group("pe", [lambda e, k=k, hd=hd: e.matmul(
                    pz.t[:, :], lhsT=win[:, k, D + hd * 128:D + (hd + 1) * 128], rhs=self.XNT.t[:, k, :],
                    start=(k == 0), stop=(k == KD - 1)) for k in range(KD)],
                    R=[self.wb, self.XNT.b], W=[pz.b])
                T.op("act", lambda e: e.activation(out=A[0], in_=pz.t[:, :], func=AF.Sigmoid), R=[pz.b], W=[ab[0]])
                T.op("dve", lambda e, hd=hd: e.tensor_scalar(out=A[1], in0=A[0], scalar1=LBT.t[:, 8 + hd:9 + hd], scalar2=LBT.t[:, hd:hd + 1],
                                                            op0=ALU.mult, op1=ALU.add), R=[ab[0], LBT.b], W=[ab[1]])
                T.op("act", lambda e: e.activation(out=A[0], in_=A[1], func=AF.Ln), R=[ab[1]], W=[ab[0]])
                T.op("dve", lambda e: e.tensor_tensor_scan(out=A[2], data0=RM, data1=A[0], initial=0.0, op0=ALU.mult, op1=ALU.add),
                     R=[self.RMK.b, ab[0]], W=[ab[2]])
                T.op("pool", lambda e: e.tensor_scalar(out=A[1], in0=A[1], scalar1=-1.0, scalar2=1.0, op0=ALU.mult, op1=ALU.add),
                     R=[ab[1]], W=[ab[1]])
                T.op("act", lambda e: e.activation(out=A[0], in_=A[2], func=AF.Exp), R=[ab[2]], W=[ab[0]])
                T.op("act", lambda e: e.activation(out=A[3], in_=A[2], func=AF.Exp, scale=-1.0), R=[ab[2]], W=[ab[3]])
                T.op("dve", lambda e, hd=hd: e.tensor_tensor(out=QTL[:, hd, :], in0=pq.t[:, :], in1=A[0], op=ALU.mult),
                     R=[pq.b, ab[0]], W=[qtb[hd]])
                T.op("pool", lambda e, hd=hd: e.tensor_copy(out=ELAST[:, hd, :], in_=A[0][:, 63:512:64]), R=[ab[0]], W=[elb])
                T.op("pool", lambda e, hd=hd: e.tensor_tensor(out=KTL[:, hd, :], in0=A[1], in1=A[3], op=ALU.mult),
                     R=[ab[1], ab[3]], W=[ktb[hd]])
                c3 = A[2].rearrange("p (c j) -> p c j", j=64)
                T.op("dve", lambda e, c3=c3: e.tensor_tensor(out=A[3].rearrange("p (c j) -> p c j", j=64),
                                                            in0=c3[:, :, 63:64].broadcast_to([128, 8, 64]), in1=c3, op=ALU.subtract),
                     R=[ab[2], ab[3]], W=[ab[3]])
                T.op("act", lambda e: e.activation(out=A[3], in_=A[3], func=AF.Exp), R=[ab[3]], W=[ab[3]])
                T.op("pool", lambda e, hd=hd: e.tensor_tensor(out=KHT[:, hd, :], in0=A[1], in1=A[3], op=ALU.mult),
                     R=[ab[1], ab[3]], W=[khtb[hd]])
            for pr in range(4):
                ts_ = slice(pr * 128, (pr + 1) * 128)
                for hd in range(8):
                    q4 = slice((hd % 4) * 128, (hd % 4 + 1) * 128)
                    pscr = self.PS[7]
                    T.op("pe", lambda e, hd=hd, q4=q4: e.matmul(pscr.t[:, q4], lhsT=KTL[:, hd, ts_], rhs=QTL[:, hd, ts_], start=True, stop=True),
                         R=[ktb[hd], qtb[hd]], W=[pscr.b])
                    T.op("dve", lambda e, hd=hd, q4=q4: e.tensor_tensor(out=ATL[:, hd, :], in0=pscr.t[:, q4], in1=BDM, op=ALU.mult),
                         R=[pscr.b, self.BDM.b], W=[atb[hd]])
                    ptr = self.PS[3]
                    T.op("pe", lambda e, hd=hd, q4=q4: e.transpose(out=ptr.t[:, q4], in_=KHT[:, hd, ts_], identity=self.IDF.t[:, :]),
                         R=[khtb[hd], self.IDF.b], W=[ptr.b])
                    T.op("act", lambda e, hd=hd, q4=q4: e.copy(out=KHS[:, hd, :], in_=ptr.t[:, q4]), R=[ptr.b], W=[khsb[hd]])
                for half in range(2):
                    for hd in range(8):
                        po = self.PS[4 + hd // 4]
                        q4 = (hd % 4) * 128
                        cc = pr * 2 + half
                        tc = slice(cc * 64, (cc + 1) * 64)
                        rr = slice(half * 64, (half + 1) * 64)
                        fns = []
                        if half == 0:
                            fns.append(lambda e, hd=hd, po=po, q4=q4: e.matmul(
                                po.t[:, q4:q4 + 128], lhsT=VSB[:, pr, hd * 128:(hd + 1) * 128], rhs=ATL[:, hd, :],
                                start=(hd % 4 == 0), stop=False, skip_group_check=True))
                        fns.append(lambda e, hd=hd, po=po, q4=q4, half=half, tc=tc: e.matmul(
                            po.t[:, q4 + half * 64:q4 + (half + 1) * 64], lhsT=SBF[:, hd, :], rhs=QTL[:, hd, tc],
                            start=False, stop=(half == 1 and hd % 4 == 3), skip_group_check=True))
                        T.group("pe", fns, R=[vsb, atb[hd], sbfb[hd], qtb[hd]], W=[po.b])
                        pkv = self.PS[6]
                        kq = slice((hd % 4) * 128, (hd % 4 + 1) * 128)
                        T.op("pe", lambda e, hd=hd, rr=rr, kq=kq: e.matmul(
                            pkv.t[:, kq], lhsT=KHS[rr, hd, :], rhs=VSB[rr, pr, hd * 128:(hd + 1) * 128], start=True, stop=True),
                            R=[khsb[hd], vsb], W=[pkv.b])
                        T.op("dve", lambda e, hd=hd, kq=kq, cc=cc: e.scalar_tensor_tensor(
                            out=SF[:, hd, :], in0=SF[:, hd, :], scalar=ELAST[:, hd, cc:cc + 1], in1=pkv.t[:, kq],
                            op0=ALU.mult, op1=ALU.add), R=[sfb[hd], elb, pkv.b], W=[sfb[hd]])
                        T.op("act", lambda e, hd=hd: e.copy(out=SBF[:, hd, :], in_=SF[:, hd, :]), R=[sfb[hd]], W=[sbfb[hd]])
                for b in range(2):
                    po = self.PS[4 + b]
                    self.evac(b, OSB[:, 4 * b:4 * b + 4, ts_], po.t[:, :].rearrange("p (h t) -> p h t", h=4),
                              R=[po.b], W=osb[4 * b:4 * b + 4])
            for hd in range(8):
                sl = hd % 2
                T.op("act", lambda e, hd=hd, sl=sl: e.activation(out=self.SQ.t[:, sl, :], in_=OSB[:, hd, :], func=AF.Square),
                     R=[osb[hd]], W=[self.SQ.bs[sl]])
                pst = self.PS[0]
                T.op("pe", lambda e, sl=sl: e.matmul(pst.t[:, :], lhsT=self.ONES.t[:, :], rhs=self.SQ.t[:, sl, :], start=True, stop=True),
                     R=[self.ONES.b, self.SQ.bs[sl]], W=[pst.b])
                rs = self.RS.t[:, 0, :]
                T.op("act", lambda e: e.activation(out=rs, in_=pst.t[:, :], func=AF.Sqrt, scale=1.0 / 128, bias=EPS), R=[pst.b], W=[self.RS.bs[0]])
                T.op("dve", lambda e: e.reciprocal(out=rs, in_=rs), R=[self.RS.bs[0]], W=[self.RS.bs[0]])
                pg = self.PS[1 + sl]
                T.group("pe", [lambda e, k=k, hd=hd, pg=pg: e.matmul(
                    pg.t[:, :], lhsT=win[:, k, 3 * D + hd * 128:3 * D + (hd + 1) * 128], rhs=self.XNT.t[:, k, :],
                    start=(k == 0), stop=(k == KD - 1)) for k in range(KD)],
                    R=[self.wb, self.XNT.b], W=[pg.b])
                T.op("act", lambda e, pg=pg: e.activation(out=self.SG.t[:, 0, :], in_=pg.t[:, :], func=AF.Silu), R=[pg.b], W=[self.SG.bs[0]])
                T.op("dve", lambda e, hd=hd: e.scalar_tensor_tensor(
                    out=self.SG.t[:, 1, :], in0=OSB[:, hd, :], scalar=self.GA.t[:, G_ON:G_ON + 1], in1=rs,
                    op0=ALU.mult, op1=ALU.mult), R=[osb[hd], self.GA.b, self.RS.bs[0]], W=[self.SG.bs[1]])
                T.op("dve", lambda e, hd=hd: e.tensor_tensor(out=OG[:, hd, :], in0=self.SG.t[:, 1, :], in1=self.SG.t[:, 0, :], op=ALU.mult),
                     R=self.SG.bs, W=[ogb])
            for m in range(KD):
                py = self.PS[1 + (m % 2)]
                T.group("pe", [lambda e, k=k, m=m, py=py: e.matmul(
                    py.t[:, :], lhsT=wout[:, k, m * 128:(m + 1) * 128], rhs=OG[:, k, :],
                    start=(k == 0), stop=(k == KD - 1)) for k in range(KD)],
                    R=[self.wb, ogb], W=[py.b])
                self.evac(m, self.HY.t[:, m, :], py.t[:, :], R=[py.b], W=[self.HY.b])
            self.postnorm_store(it, self.GA, gpost, self.PS[0])


    def phase_out(self):
        T = self.T
        for it in range(NT):
            T.dma("sp", self.HY.t[:, :, :], self.hb_tile(it), R=[self.hb[it]], W=[self.HY.b])
            for s in range(4):
                t0 = it * TT + s * 128
                sl = s % 2
                for half in range(2):
                    ps = self.PS[half]
                    T.group("pe", [lambda e, j=j, half=half, s=s, ps=ps: e.transpose(
                        out=ps.t[:, j * 128:(j + 1) * 128],
                        in_=self.HY.t[:, half * 4 + j, s * 128:(s + 1) * 128],
                        identity=self.IDF.t[:, :]) for j in range(4)],
                        R=[self.HY.b, self.IDF.b], W=[ps.b])
                    dst = self.XIN.t[:, sl, half * 512:(half + 1) * 512]
                    if half == 0:
                        T.op("act", lambda e, dst=dst, ps=ps: e.copy(out=dst, in_=ps.t[:, :]), R=[ps.b], W=[self.XIN.bs[sl]])
                    else:
                        T.op("dve", lambda e, dst=dst, ps=ps: e.tensor_copy(out=dst, in_=ps.t[:, :]), R=[ps.b], W=[self.XIN.bs[sl]])
                T.dma("sp", self.out[t0:t0 + 128, :], self.XIN.t[:, sl, :], R=[self.XIN.bs[sl]], W=[Buf()])

    def phase_ffn(self, j, layer):
        T = self.T
        NI = 2 * DFF
        WIN = 0
        WOUT = KD * NI
        self.load_w(WIN, self.ffn_w_in[j][layer], KD, NI, nsplit=8)
        self.load_w(WOUT, self.ffn_w_out[j][layer], KF, D, nsplit=2)
        win = self.WBIG[:, WIN:WIN + KD * NI].rearrange("p (k n) -> p k n", k=KD)
        wout = self.WBIG[:, WOUT:WOUT + KF * D].rearrange("p (k n) -> p k n", k=KF)
        gpre = gidx(f"ffn{j + 1}_norm_pre", layer)
        gpost = (j * 2 + layer) * KD
        for it in range(NT):
            T.dma("sp", self.HY.t[:, :, :], self.hb_tile(it), R=[self.hb[it]], W=[self.HY.b])
            self.prenorm(gpre, self.PS[0])
            for n in range(KF):
                pg = self.PS[1 + (n % 2)]
                pu = self.PS[3 + (n % 2)]
                T.group("pe", [lambda e, k=k, n=n, pg=pg: e.matmul(
                    pg.t[:, :], lhsT=win[:, k, n * 128:(n + 1) * 128], rhs=self.XNT.t[:, k, :],
                    start=(k == 0), stop=(k == KD - 1)) for k in range(KD)],
                    R=[self.wb, self.XNT.b], W=[pg.b])
                T.group("pe", [lambda e, k=k, n=n, pu=pu: e.matmul(
                    pu.t[:, :], lhsT=win[:, k, DFF + n * 128:DFF + (n + 1) * 128], rhs=self.XNT.t[:, k, :],
                    start=(k == 0), stop=(k == KD - 1)) for k in range(KD)],
                    R=[self.wb, self.XNT.b], W=[pu.b])
                sl = n % 2
                T.op("act", lambda e, sl=sl, pg=pg: e.activation(out=self.SG.t[:, sl, :], in_=pg.t[:, :], func=AF.Silu),
                     R=[pg.b], W=[self.SG.bs[sl]])
                T.op("dve", lambda e, sl=sl, pu=pu, n=n: e.tensor_tensor(
                    out=self.HID.t[:, n, :], in0=self.SG.t[:, sl, :], in1=pu.t[:, :], op=ALU.mult),
                    R=[self.SG.bs[sl], pu.b], W=[self.HID.bs[n]])
            for m in range(KD):
                py = self.PS[5 + (m % 2)]
                T.group("pe", [lambda e, k=k, m=m, py=py: e.matmul(
                    py.t[:, :], lhsT=wout[:, k, m * 128:(m + 1) * 128], rhs=self.HID.t[:, k, :],
                    start=(k == 0), stop=(k == KF - 1)) for k in range(KF)],
                    R=[self.wb] + self.HID.bs, W=[py.b])
                if m % 2 == 0:
                    T.op("act", lambda e, m=m, py=py: e.copy(out=self.HY.t[:, m, :], in_=py.t[:, :]), R=[py.b], W=[self.HY.b])
                else:
                    T.op("dve", lambda e, m=m, py=py: e.tensor_copy(out=self.HY.t[:, m, :], in_=py.t[:, :]), R=[py.b], W=[self.HY.b])
            self.postnorm_store(it, self.GH, gpost, self.PS[7])

    def build(self):
        T = self.T
        self.setup()
        self.phase_in()
        T.barrier()
        for ph in self.cfg["phases"]:
            if ph[0] == "ffn":
                self.phase_ffn(ph[1], ph[2])
            elif ph[0] == "ple":
                self.phase_ple(ph[1])
            elif ph[0] == "kvf":
                self.phase_kvf()
            elif ph[0] == "fox":
                self.phase_fox()
            elif ph[0] == "hgrn":
                self.phase_hgrn()
            elif ph[0] == "reload":
                self.phase_in(self.x2)
            T.barrier()
        self.phase_out()
        T.barrier()
        return self.nc


def pack_gains(inp):
    g = np.zeros((128, G_TOT), np.float32)
    for i, nm in enumerate(GAIN_NAMES):
        for layer in range(2):
            g[:, (i * 2 + layer) * KD:(i * 2 + layer + 1) * KD] = inp[nm][layer].reshape(KD, 128).T
    g[:, G_KV:G_KV + KD] = inp["kv_norm"].reshape(KD, 128).T
    g[:, G_LB0:G_LB0 + KD] = inp["hgrn_lb_logits"][0].reshape(KD, 128).T
    g[:, G_LB1:G_LB1 + KD] = inp["hgrn_lb_logits"][1].reshape(KD, 128).T
    g[:, G_ON] = inp["hgrn_out_norm"][0]
    return g


DEFAULT_CFG = dict(phases=[("ffn", 0, 0), ("hgrn",), ("ffn", 1, 0), ("ple", 0), ("kvf",),
                           ("ffn", 0, 1), ("fox",), ("ffn", 1, 1), ("ple", 1)])


def kernel(_cfg=None, **inp):
    cfg = _cfg or DEFAULT_CFG
    inp = {k: np.asarray(v) for k, v in inp.items()}
    prog = Prog(cfg)
    nc = prog.build()
    gains = pack_gains(inp)
    bf_bc = np.ascontiguousarray(np.broadcast_to(inp["fox_b_f"].astype(np.float32)[None, :], (128, 16)))
    ident = np.eye(128, dtype=np.float32)
    shared = {k: np.ascontiguousarray(inp[k]) for k in
              ["ffn1_w_in", "ffn1_w_out", "ffn2_w_in", "ffn2_w_out", "hgrn_w_in", "hgrn_w_out",
               "fox_w_kvf", "fox_w_qg", "fox_w_out", "ple_w_gate", "ple_w_proj"]}
    cm = np.zeros((128, 4, 128), np.float32)
    ii = np.arange(128)
    cm[:, 0, :] = (ii[:, None] <= ii[None, :])
    cm[:, 1, :] = 1.0
    cm[64, 2, 0:64] = 1.0
    cm[0, 2, 64:128] = 1.0
    cm[:, 3, :] = np.where(ii[:, None] > ii[None, :], -30000.0, 0.0)
    rm = np.ones((128, TT + 128), np.float32)
    rm[:, 0:TT:64] = 0.0
    rm[:, TT:] = ((ii[:, None] // 64) == (ii[None, :] // 64)) & (ii[:, None] <= ii[None, :])
    shared.update(gains=gains, bf_bc=bf_bc, ident=ident, cmat=cm, rmask=rm)
    ncores = cfg.get("ncores", NCORES)
    in_maps = []
    for c in range(ncores):
        m = dict(shared)
        m["x"] = np.ascontiguousarray(inp["x"][c])
        m["p"] = np.ascontiguousarray(inp["p"][:, c])
        if cfg.get("x2") is not None:
            m["x2"] = np.ascontiguousarray(cfg["x2"][c])
        in_maps.append(m)
    if cfg.get("trace"):
        res = run_bass_kernel_spmd(nc, in_maps, core_ids=list(range(ncores)), trace=True)
        cfg["_exec_ns"] = res.exec_time_ns
    else:
        res = run_bass_kernel_spmd(nc, in_maps, core_ids=list(range(ncores)))
    if cfg.get("debug"):
        cfg["_res"] = res.results
    out = np.stack([np.asarray(r["out"]) for r in res.results], axis=0)
    return out.astype(np.float32)
```
